# Optimizing a Trainium2 kernel written in Bass

```python
import jax, jax.numpy as jnp
from jax import lax
import numpy as np

D_MODEL = 2048
BATCH = 1
SEQ = 16384
DEPTH = 1

CHUNK = 64
SGU_BLOCK = 128
SGU_HEADS = 8
SGU_WIDTH = D_MODEL
SGU_HEAD_DIM = SGU_WIDTH // SGU_HEADS
POOL_WINDOWS = (2, 4, 8, 16)
POOL_GROUPS = len(POOL_WINDOWS)
POOL_WIDTH = D_MODEL
POOL_GROUP_DIM = POOL_WIDTH // POOL_GROUPS
N_BRANCHES = 2
D_FF = 5632
IN_COLS = 2 * SGU_WIDTH + POOL_WIDTH + N_BRANCHES * D_MODEL
RMS_EPS = 1e-6
LN_EPS = 1e-5

kernel_name = "hybrid_sgu_pool_gated_macaron"


def _rmsnorm(x, g):
    xf = x.astype(jnp.float32)
    y = xf * lax.rsqrt(jnp.mean(xf * xf, axis=-1, keepdims=True) + RMS_EPS)
    return (y * g.astype(jnp.float32)).astype(x.dtype)


def _layernorm(x, g, b):
    xf = x.astype(jnp.float32)
    mu = jnp.mean(xf, axis=-1, keepdims=True)
    var = jnp.mean(jnp.square(xf - mu), axis=-1, keepdims=True)
    y = (xf - mu) * lax.rsqrt(var + LN_EPS)
    return (y * g.astype(jnp.float32) + b.astype(jnp.float32)).astype(x.dtype)


def _swiglu(x, w_in, w_out):
    gate, up = jnp.split(x @ w_in, 2, axis=-1)
    return (jax.nn.silu(gate) * up) @ w_out


def _spatial_gating(z, ln_g, ln_b, w_s, b_s):
    bsz, s_len, _ = z.shape
    u, v = jnp.split(jax.nn.gelu(z, approximate=False), 2, axis=-1)
    v = _layernorm(v, ln_g, ln_b)
    v = v.reshape(bsz, s_len // SGU_BLOCK, SGU_BLOCK, SGU_HEADS, SGU_HEAD_DIM)
    pos = jnp.arange(SGU_BLOCK)
    mask = (pos[None, :] // CHUNK) <= (pos[:, None] // CHUNK)
    w = w_s * mask.astype(w_s.dtype)[None]
    s = jnp.einsum('hij,bnjhd->bnihd', w, v) + b_s.T[:, :, None]
    return u * s.reshape(bsz, s_len, SGU_WIDTH)


def _multiscale_pool(p, pool_w, pool_scale):
    bsz, s_len, _ = p.shape
    pf = p.astype(jnp.float32)
    csum = jnp.concatenate([jnp.zeros((bsz, 1, POOL_WIDTH), jnp.float32),
                            jnp.cumsum(pf, axis=1)], axis=1)
    upper = csum[:, 1:]
    t = jnp.arange(s_len)
    outs = []
    for k, w in enumerate(POOL_WINDOWS):
        sl = slice(k * POOL_GROUP_DIM, (k + 1) * POOL_GROUP_DIM)
        lower = jnp.pad(csum[:, :s_len + 1 - w, sl], ((0, 0), (w - 1, 0), (0, 0)))
        cnt = jnp.minimum(t + 1, w).astype(jnp.float32)[None, :, None]
        outs.append((upper[..., sl] - lower) / cnt - pf[..., sl])
    pooled = jnp.stack(outs, axis=2).astype(p.dtype)
    mixed = jnp.einsum('bsgc,gcd->bsgd', pooled, pool_w)
    return mixed.reshape(bsz, s_len, POOL_WIDTH) * pool_scale


def setup_inputs(seed: int = 0) -> dict:
    key = jax.random.key(seed)
    ks = jax.random.split(key, 24)
    f32 = jnp.float32

    def nrm(k, shape, fan_in):
        return jax.random.normal(k, shape, f32) * (fan_in ** -0.5)

    def gain(k, shape):
        return 1.0 + 0.02 * jax.random.normal(k, shape, f32)

    return {
        "x": jax.random.normal(ks[0], (BATCH, SEQ, D_MODEL), f32),
        "ffn1_norm": gain(ks[1], (D_MODEL,)),
        "ffn1_w_in": nrm(ks[2], (D_MODEL, 2 * D_FF), D_MODEL),
        "ffn1_w_out": nrm(ks[3], (D_FF, D_MODEL), D_FF),
        "mix_norm": gain(ks[4], (D_MODEL,)),
        "w_in": nrm(ks[5], (D_MODEL, IN_COLS), D_MODEL),
        "b_in": 0.02 * jax.random.normal(ks[6], (IN_COLS,), f32),
        "sgu_ln_g": gain(ks[7], (SGU_WIDTH,)),
        "sgu_ln_b": 0.02 * jax.random.normal(ks[8], (SGU_WIDTH,), f32),
        "sgu_w_s": nrm(ks[9], (SGU_HEADS, SGU_BLOCK, SGU_BLOCK), SGU_BLOCK),
        "sgu_b_s": gain(ks[10], (SGU_HEADS, SGU_BLOCK)),
        "pool_w": nrm(ks[11], (POOL_GROUPS, POOL_GROUP_DIM, POOL_GROUP_DIM), POOL_GROUP_DIM),
        "pool_scale": gain(ks[12], (POOL_WIDTH,)),
        "w_branch_a": nrm(ks[13], (SGU_WIDTH, D_MODEL), SGU_WIDTH),
        "w_branch_b": nrm(ks[14], (POOL_WIDTH, D_MODEL), POOL_WIDTH),
        "w_out": nrm(ks[15], (D_MODEL, D_MODEL), D_MODEL),
        "ffn2_norm": gain(ks[16], (D_MODEL,)),
        "ffn2_w_in": nrm(ks[17], (D_MODEL, 2 * D_FF), D_MODEL),
        "ffn2_w_out": nrm(ks[18], (D_FF, D_MODEL), D_FF),
        "final_norm": gain(ks[19], (D_MODEL,)),
    }


def reference(x, ffn1_norm, ffn1_w_in, ffn1_w_out, mix_norm, w_in, b_in,
              sgu_ln_g, sgu_ln_b, sgu_w_s, sgu_b_s, pool_w, pool_scale,
              w_branch_a, w_branch_b, w_out, ffn2_norm, ffn2_w_in, ffn2_w_out,
              final_norm):
    h = x
    for _ in range(DEPTH):
        h = h + 0.5 * _swiglu(_rmsnorm(h, ffn1_norm), ffn1_w_in, ffn1_w_out)

        n = _rmsnorm(h, mix_norm)
        proj = n @ w_in + b_in
        o_a = 2 * SGU_WIDTH
        o_b = o_a + POOL_WIDTH
        z_a = proj[..., :o_a]
        z_b = proj[..., o_a:o_b]
        gate_a = jax.nn.sigmoid(proj[..., o_b:o_b + D_MODEL])
        gate_b = jax.nn.sigmoid(proj[..., o_b + D_MODEL:])

        y_a = _spatial_gating(z_a, sgu_ln_g, sgu_ln_b, sgu_w_s, sgu_b_s) @ w_branch_a
        y_b = _multiscale_pool(z_b, pool_w, pool_scale) @ w_branch_b
        merged = gate_a * y_a + gate_b * y_b
        h = h + merged @ w_out

        h = h + 0.5 * _swiglu(_rmsnorm(h, ffn2_norm), ffn2_w_in, ffn2_w_out)
    return _rmsnorm(h, final_norm).astype(x.dtype)
```

```python
import numpy as np
import concourse.bass as bass
import concourse.mybir as mybir
from concourse.bass_utils import run_bass_kernel_spmd

F32 = mybir.dt.float32
BF16 = mybir.dt.bfloat16
AF = mybir.ActivationFunctionType
ALU = mybir.AluOpType

COMPUTE = ("pe", "act", "dve")
QUEUES = ("sp", "pool")


class Buf:
    __slots__ = ("name", "writer", "readers")

    def __init__(self, name):
        self.name = name
        self.writer = None
        self.readers = {}


class Op:
    __slots__ = ("eng", "fn", "deps", "signal", "count", "sem", "val")

    def __init__(self, eng, fn):
        self.eng = eng
        self.fn = fn
        self.deps = []
        self.signal = False
        self.count = 0
        self.sem = None
        self.val = 0


class Prog:
    def __init__(self):
        self.ops = {e: [] for e in COMPUTE + QUEUES}
        self.dma_counts = {}

    def add(self, eng, fn, reads=(), writes=(), dma_sem=None, extra_deps=()):
        op = Op(eng, fn)
        is_dma = dma_sem is not None
        deps = []
        for b in reads:
            if b.writer is not None:
                deps.append((b.writer, True))
        for b in writes:
            if b.writer is not None:
                deps.append((b.writer, False))
            for r in b.readers.values():
                deps.append((r, False))
        for d in extra_deps:
            deps.append((d, True))
        seen = set()
        for d, raw in deps:
            if d is op or id(d) in seen:
                continue
            if d.sem is None and d.eng == eng and not is_dma:
                if eng == "pe":
                    continue
            seen.add(id(d))
            op.deps.append(d)
            if d.sem is None:
                d.signal = True
        if is_dma:
            n = self.dma_counts.get(id(dma_sem), 0) + 1
            self.dma_counts[id(dma_sem)] = n
            op.sem = dma_sem
            op.val = 16 * n
        for b in reads:
            b.readers[eng if not is_dma else ("dma", id(op))] = op
        for b in writes:
            b.writer = op
            b.readers = {}
        self.ops[eng].append(op)
        return op

    def emit(self, nc, block, sems):
        for e in COMPUTE:
            c = 0
            for op in self.ops[e]:
                if op.signal:
                    c += 1
                op.count = c
        prog = self

        def run(engname, eng):
            waited = {}
            embed = engname in COMPUTE
            for op in prog.ops[engname]:
                need = {}
                for d in op.deps:
                    if d.sem is not None:
                        s, v = d.sem, d.val
                    else:
                        s, v = sems[d.eng], d.count
                    if waited.get(id(s), 0) < v:
                        waited[id(s)] = v
                        need[id(s)] = (s, v)
                need = list(need.values())
                last = need.pop() if (need and embed and op.fn is not None) else None
                for s, v in need:
                    eng.wait_ge(s, v)
                if op.fn is None:
                    continue
                ins = op.fn(eng)
                if last is not None:
                    ins._wait_ge(last[0], last[1])
                if op.sem is not None:
                    ins.then_inc(op.sem, 16)
                elif op.signal:
                    ins.then_inc(sems[engname], 1)

        @block.tensor
        def _(t):
            run("pe", t)

        @block.scalar
        def _(a):
            run("act", a)

        @block.vector
        def _(v):
            run("dve", v)

        @block.sync
        def _(s):
            run("sp", s)

        @block.gpsimd
        def _(g):
            run("pool", g)


NCORES = 8
D = 2048
SEQ = 16384
TOK = SEQ // NCORES
T = 512
NT = TOK // T
HW = 16
DFF = 5632
KC = D // 128
MC = DFF // 128
NSLOT = 4
SLOT_BLKS = 32
ITEMS_PER_PASS = 198
RMS_EPS = 1e-6
LN_EPS = 1e-5
POOL_WINDOWS = (2, 4, 8, 16)

V_FFN1, V_MIX, V_FFN2, V_FIN, V_BU, V_BGA, V_BGB, V_LNG, V_LNB, V_PSC = [16 * i for i in range(10)]
NV = 160


class Seg:
    def __init__(self, w, hT, nbT, a_fn, hbufs, nbbufs, abufs, rs, rsbuf):
        self.w = w
        self.hT = hT
        self.nbT = nbT
        self.a_fn = a_fn
        self.hbufs = hbufs
        self.nbbufs = nbbufs
        self.abufs = abufs
        self.rs = rs
        self.rsbuf = rsbuf

    def h(self, kc):
        return self.hT[:, kc, 0:self.w]

    def nb(self, kc):
        return self.nbT[:, kc, 0:self.w]

    def a(self, m):
        return self.a_fn(m)


def build_program():
    nc = bass.Bass("TRN2", target_bir_lowering=False)
    P = Prog()
    xT = nc.dram_tensor("xT", [D, TOK], F32, kind="ExternalInput").ap()
    xh = nc.dram_tensor("xh", [D, HW], F32, kind="ExternalInput").ap()
    wst = nc.dram_tensor("wstream", [ITEMS_PER_PASS, 128, SLOT_BLKS * 128], F32, kind="ExternalInput").ap()
    vecs_d = nc.dram_tensor("vecs", [128, NV], F32, kind="ExternalInput").ap()
    brow_d = nc.dram_tensor("brow", [16, 512], F32, kind="ExternalInput").ap()
    sel_d = nc.dram_tensor("sel", [16, 8 * 128], F32, kind="ExternalInput").ap()
    mask_d = nc.dram_tensor("mask16", [16, 2], F32, kind="ExternalInput").ap()
    wsT_d = nc.dram_tensor("wsT", [128, 8 * 128], F32, kind="ExternalInput").ap()
    bs_d = nc.dram_tensor("bsrow", [8 * 128], F32, kind="ExternalInput").ap()
    pm_d = nc.dram_tensor("pmats", [128, 16 * 128], F32, kind="ExternalInput").ap()
    outT = nc.dram_tensor("outT", [D, TOK], F32, kind="ExternalOutput").ap()

    A = nc.alloc_sbuf_tensor
    hTs = [A("hT0", [128, KC, T], F32), A("hT1", [128, KC, T], F32)]
    nbT = A("nbT", [128, KC, T], BF16)
    R = A("R", [128, 48 * 256], F32)
    slots = [A(f"slot{i}", [128, SLOT_BLKS * 128], BF16) for i in range(NSLOT)]
    hT_h = A("hT_h", [128, KC, HW], F32)
    nbT_h = A("nbT_h", [128, KC, HW], BF16)
    aT_h = A("aT_h", [128, MC, HW], BF16)
    NTMP = 4
    tmps = [A(f"tmp{i}", [128, T], F32) for i in range(NTMP)]
    rs_m = A("rs_m", [128, T], F32)
    rs_h = A("rs_h", [128, HW], F32)
    rs_f = A("rs_f", [128, T], F32)
    beta = A("beta", [128, KC, 128], F32)
    wm_bf = A("wm_bf", [128, 8, 128], BF16)
    PTstd = A("PTstd", [128, 8, 128], BF16)
    PThi = A("PThi", [128, 8, 128], BF16)
    PTlo = A("PTlo", [128, 8, 128], BF16)
    ones_bf = A("ones_bf", [128, 128], BF16)
    ones_f = A("ones_f", [128, 128], F32)
    vecs = A("vecs_sb", [128, NV], F32)
    cst = A("cst", [128, 2], F32)
    brow_bf = A("brow_bf", [128, 512], BF16)
    sel_bf = A("sel_bf", [128, 8 * 128], BF16)
    mask_sb = A("mask_sb", [16, 2], F32)
    zb_prev = A("zb_prev", [128, D], BF16)
    vstats = A("vstats", [128, 4, 4, 6], F32)
    mv = A("mv", [128, 4, 2], F32)
    rstd = A("rstd", [128, 4], F32)
    warm = A("warm", [128, 2], F32)

    pss = [nc.alloc_psum_tensor(f"ps{i}", [128, T], F32) for i in range(8)]

    hbs = [[Buf(f"h{p}_{k}") for k in range(KC)] for p in range(2)]
    nbb = [Buf(f"nb{k}") for k in range(KC)]
    Rb = [Buf(f"R{u}") for u in range(48)]
    slotb = [Buf(f"slot{i}") for i in range(NSLOT)]
    hb_h = [Buf(f"hh{k}") for k in range(KC)]
    nbb_h = [Buf(f"nbh{k}") for k in range(KC)]
    ab_h = [Buf(f"ah{m}") for m in range(MC)]
    tmpb = [Buf(f"tmp{i}") for i in range(NTMP)]
    psb = [Buf(f"ps{i}") for i in range(8)]
    B = {k: Buf(k) for k in ["rs_m", "rs_h", "rs_f", "beta", "wm", "PTstd", "PThi", "PTlo", "ones_bf", "ones_f", "vecs",
                              "cst", "brow", "sel", "mask", "zb_prev"]}
    vstb = [Buf(f"vst{i}") for i in range(4)]
    warmb = Buf("warm")
    mvb = [Buf(f"mv{i}") for i in range(4)]

    def Rbf(u0, n=1):
        return R[:, u0 * 256:(u0 + n) * 256].bitcast(BF16)

    def Rf32(u0, n=1):
        return R[:, u0 * 256:(u0 + n) * 256]

    seg_ms = [Seg(T, hTs[p], nbT, lambda m: Rbf(m), hbs[p], nbb, Rb, rs_m, B["rs_m"]) for p in range(2)]
    seg_h = Seg(HW, hT_h, nbT_h, lambda m: aT_h[:, m, :], hb_h, nbb_h, ab_h, rs_h, B["rs_h"])

    from contextlib import ExitStack
    with ExitStack() as es:
        sems = {e: es.enter_context(nc.semaphore(f"s_{e}")) for e in COMPUTE}
        wsems = [es.enter_context(nc.semaphore(f"s_w{i}")) for i in range(NSLOT)]
        xsems = [es.enter_context(nc.semaphore(f"s_x{i}")) for i in range(2)]
        osems = [es.enter_context(nc.semaphore(f"s_o{i}")) for i in range(2)]
        csems = [es.enter_context(nc.semaphore(f"s_c{i}")) for i in range(9)]
        x0sems = [es.enter_context(nc.semaphore(f"s_x0_{i}")) for i in range(3)]
        lsems = [es.enter_context(nc.semaphore(f"s_l{i}")) for i in range(4)]
        cctr = [0]

        def csem_next():
            cctr[0] += 1
            return csems[cctr[0] - 1]
        block = es.enter_context(nc.Block())

        def mm(out, lhsT, rhs, start, stop, reads, writes):
            return P.add("pe", lambda e: e.matmul(out, lhsT, rhs, start=start, stop=stop), reads=reads, writes=writes)

        def act(out, in_, func, reads, writes, **kw):
            return P.add("act", lambda e: e.activation(out=out, in_=in_, func=func, **kw), reads=reads, writes=writes)

        def dve(method, reads, writes, **kw):
            return P.add("dve", lambda e: getattr(e, method)(**kw), reads=reads, writes=writes)

        def dma(q, out, in_, sem, reads, writes, extra=()):
            return P.add(q, lambda e: e.dma_start(out=out, in_=in_), reads=reads, writes=writes, dma_sem=sem,
                         extra_deps=extra)

        pctr = [0]

        reserved = set()

        def psum_next():
            while True:
                i = pctr[0] % 8
                pctr[0] += 1
                if i not in reserved:
                    return pss[i], psb[i]

        def psum_reserve():
            while True:
                i = pctr[0] % 8
                pctr[0] += 1
                if i not in reserved:
                    reserved.add(i)
                    return i

        tctr = [0]

        def tmp_next():
            i = tctr[0] % NTMP
            tctr[0] += 1
            return tmps[i], tmpb[i]

        class WS:
            def __init__(self):
                self.total = ITEMS_PER_PASS * NT
                self.issued = 0
                self.pos = 0

            def issue(self):
                g = self.issued
                if g >= self.total:
                    return
                s = g % NSLOT
                dma("pool", slots[s][:, :], wst[g % ITEMS_PER_PASS], wsems[s], [], [slotb[s]],
                    extra=x0_ops[:2] if g == 0 else (x0_ops if g == 1 else ()))
                self.issued += 1

            def _loc(self):
                g, b = divmod(self.pos, SLOT_BLKS)
                assert g < self.issued, (g, self.issued)
                return g % NSLOT, b

            def take(self, n):
                out = []
                for _ in range(n):
                    s, b = self._loc()
                    out.append((slots[s][:, b * 128:(b + 1) * 128], slotb[s]))
                    self.pos += 1
                return out

            def take_wide(self):
                s, b = self._loc()
                assert b % 4 == 0
                self.pos += 4
                return slots[s][:, b * 128:(b + 4) * 128], slotb[s]

            def release(self):
                done = self.pos // SLOT_BLKS
                while self.issued < min(done + NSLOT, self.total):
                    self.issue()

        ws = WS()
        x0_ops = []

        x0v = xT[:, 0:T].rearrange("(k p) n -> p k n", p=128)
        for q in range(4):
            x0_ops.append(dma("sp", hTs[0][:, 4 * q:4 * q + 4, :], x0v[:, 4 * q:4 * q + 4, :],
                              xsems[0] if q == 0 else x0sems[q - 1], [], hbs[0][4 * q:4 * q + 4]))
        for _ in range(NSLOT):
            ws.issue()
        dma("sp", vecs[:, :], vecs_d[:, :], csem_next(), [], [B["vecs"]])
        dma("sp", hT_h[:, :, :], xh.rearrange("(k p) n -> p k n", p=128), csem_next(), [], hb_h)
        dma("sp", mask_sb[:, :], mask_d[:, :], csem_next(), [], [B["mask"]])
        SB = hbs[1]

        def stg(k0, n):
            return hTs[1][:, k0:k0 + n, :].rearrange("p a b -> p (a b)")

        pm32 = stg(0, 4).rearrange("p (k n) -> p k n", k=16)
        ws32 = stg(4, 2).rearrange("p (h n) -> p h n", h=8)
        bs32 = stg(6, 2).rearrange("p (h n) -> p h n", h=8)
        br32 = stg(8, 1)[0:16, :]
        brhi = stg(9, 1)[0:16, :]
        brd = stg(10, 1)[0:16, :]
        sel32 = stg(11, 2)[0:16, :]
        pmhi32 = stg(13, 2).rearrange("p (k n) -> p k n", k=8)
        dma("sp", stg(0, 4), pm_d[:, :], csem_next(), [], SB[0:4])
        dma("sp", stg(4, 2), wsT_d[:, :], csem_next(), [], SB[4:6])
        dma("sp", stg(6, 2), bs_d.partition_broadcast(128), csem_next(), [], SB[6:8])
        dma("sp", br32, brow_d[:, :], csem_next(), [], SB[8:9])
        dma("sp", sel32, sel_d[:, :], csem_next(), [], SB[11:13])

        dve("memset", [], [B["ones_bf"]], ap=ones_bf[:, :], constant=1.0)
        dve("memset", [], [B["cst"]], ap=cst[:, 0:1], constant=RMS_EPS)
        dve("memset", [], [B["cst"]], ap=cst[:, 1:2], constant=LN_EPS)
        dve("memset", [], [warmb], ap=warm[:, :], constant=1.0)

        def setup_part2():
            dve("memset", [], [B["ones_f"]], ap=ones_f[:, :], constant=1.0)
            dve("memset", [], [B["zb_prev"]], ap=zb_prev[:, :], constant=0.0)
            dve("tensor_copy", SB[0:4], [B["PTstd"]], out=PTstd[:, :, :], in_=pm32[:, 0:8, :])
            dve("tensor_copy", SB[0:4], [B["PThi"]], out=PThi[:, :, :], in_=pm32[:, 8:16, :])
            dve("tensor_copy", [B["PThi"]], SB[13:15], out=pmhi32, in_=PThi[:, :, :])
            dve("tensor_tensor", SB[0:4] + SB[13:15], [B["PTlo"]], out=PTlo[:, :, :], in0=pm32[:, 8:16, :], in1=pmhi32,
                op=ALU.subtract)
            dve("memset", [], SB[4:6], ap=ws32[64:128, :, 0:64], constant=0.0)
            dve("tensor_copy", SB[4:6], [B["wm"]], out=wm_bf[:, :, :], in_=ws32)
            dve("memset", [], [B["brow"]], ap=brow_bf[:, :], constant=0.0)
            dve("memset", [], [B["sel"]], ap=sel_bf[:, :], constant=0.0)
            dve("tensor_copy", SB[8:9], [B["brow"]], out=brow_bf[0:16, :], in_=br32)
            dve("tensor_copy", [B["brow"]], SB[9:10], out=brhi, in_=brow_bf[0:16, :])
            dve("tensor_tensor", SB[8:10], SB[10:11], out=brd, in0=br32, in1=brhi, op=ALU.subtract)
            dve("tensor_scalar", SB[9:10] + [B["mask"]], SB[9:10], out=brhi, in0=brhi, scalar1=mask_sb[:, 0:1], scalar2=None,
                op0=ALU.mult)
            dve("scalar_tensor_tensor", SB[9:11] + [B["mask"]], SB[10:11], out=brd, in0=brd, scalar=mask_sb[:, 1:2], in1=brhi,
                op0=ALU.mult, op1=ALU.add)
            dve("tensor_copy", SB[10:11], [B["brow"]], out=brow_bf[0:16, :], in_=brd)
            dve("tensor_copy", SB[11:13], [B["sel"]], out=sel_bf[0:16, :], in_=sel32)

        def setup_part3():
            for half in range(2):
                ps, pb = psum_next()
                for hh in range(4):
                    h = half * 4 + hh
                    mm(ps[:, hh * 128:(hh + 1) * 128], ones_f[:, :], ws32[:, h, :], True, True, [B["ones_f"]] + SB[4:6], [pb])
                for hh in range(4):
                    h = half * 4 + hh
                    for c2 in range(2):
                        cc = h * 2 + c2
                        dve("scalar_tensor_tensor", [pb, B["vecs"]] + SB[6:8], [B["beta"]], out=beta[:, cc, :],
                            in0=ps[:, hh * 128:(hh + 1) * 128], scalar=vecs[:, V_LNB + cc:V_LNB + cc + 1], in1=bs32[:, h, :],
                            op0=ALU.mult, op1=ALU.add)

        def rms_squares(seg):
            for kc in range(KC):
                act(seg.nb(kc), seg.h(kc), AF.Square, [seg.hbufs[kc]], [seg.nbbufs[kc]])

        def rms_reduce(seg):
            w = seg.w
            ps, pb = psum_next()
            for kc in range(KC):
                mm(ps[:, 0:w], ones_bf[:, :], seg.nb(kc), kc == 0, kc == KC - 1, [B["ones_bf"], seg.nbbufs[kc]], [pb])
            act(seg.rs[:, 0:w], ps[:, 0:w], AF.Ln, [pb, B["cst"]], [seg.rsbuf], scale=1.0 / D, bias=cst[:, 0:1])
            act(seg.rs[:, 0:w], seg.rs[:, 0:w], AF.Exp, [seg.rsbuf], [seg.rsbuf], scale=-0.5)

        def rms_apply(seg, vcol):
            w = seg.w
            for kc in range(KC):
                dve("scalar_tensor_tensor", [seg.hbufs[kc], seg.rsbuf, B["vecs"]], [seg.nbbufs[kc]], out=seg.nb(kc),
                    in0=seg.h(kc), scalar=vecs[:, vcol + kc:vcol + kc + 1], in1=seg.rs[:, 0:w], op0=ALU.mult, op1=ALU.mult)

        def warm_sqrt():
            act(warm[:, 1:2], warm[:, 0:1], AF.Ln, [warmb], [warmb])

        mid = {}

        def mid_sq(seg, kc):
            act(seg.nb(kc), seg.h(kc), AF.Square, [seg.hbufs[kc]], [seg.nbbufs[kc]])

        def mid_red(seg, kc):
            if kc == 0:
                mid["bank"] = psum_reserve()
            i = mid["bank"]
            mm(pss[i][:, 0:seg.w], ones_bf[:, :], seg.nb(kc), kc == 0, kc == KC - 1, [B["ones_bf"], seg.nbbufs[kc]], [psb[i]])

        def mid_step(seg, f):
            mid_sq(seg, f)
            if f >= 1:
                mid_red(seg, f - 1)

        def mid_finish(seg, vcol):
            w = seg.w
            mid_red(seg, KC - 1)
            fp, fpb = psum_next()
            for j in range(6):
                mm(fp[:, :], ones_bf[:, :], PTstd[:, 0:4, :].rearrange("p a b -> p (a b)"), j == 0, j == 5,
                   [B["ones_bf"], B["PTstd"]], [fpb])
            i = mid["bank"]
            act(seg.rs[:, 0:w], pss[i][:, 0:w], AF.Ln, [psb[i], B["cst"]], [seg.rsbuf], scale=1.0 / D, bias=cst[:, 0:1])
            reserved.discard(i)
            act(seg.rs[:, 0:w], seg.rs[:, 0:w], AF.Exp, [seg.rsbuf], [seg.rsbuf], scale=-0.5)
            rms_apply(seg, vcol)

        def rmsnorm(seg, vcol):
            rms_squares(seg)
            rms_reduce(seg)
            rms_apply(seg, vcol)

        def ffn(segs, vcol, do_norm=True, hooks=None, lazy=None, inline_next=None, interleave=False):
            hooks = hooks or {}
            lazy = lazy or {}
            if do_norm:
                for seg in segs:
                    rmsnorm(seg, vcol)
            m_start = 0
            if interleave:
                seg = segs[0]
                grp = []
                for m in range(2):
                    blks = ws.take(32)
                    pg, pgb = psum_next()
                    pu, pub = psum_next()
                    grp.append((m, blks, pg, pgb, pu, pub))
                for kc in range(KC):
                    for (m, blks, pg, pgb, pu, pub) in grp:
                        mm(pg[:, :], blks[kc][0], seg.nb(kc), kc == 0, kc == KC - 1, [blks[kc][1], seg.nbbufs[kc]], [pgb])
                        mm(pu[:, :], blks[16 + kc][0], seg.nb(kc), kc == 0, kc == KC - 1,
                           [blks[16 + kc][1], seg.nbbufs[kc]], [pub])
                for (m, blks, pg, pgb, pu, pub) in grp:
                    t, tb = tmp_next()
                    act(t[:, :], pg[:, :], AF.Silu, [pgb], [tb])
                    dve("tensor_tensor", [pub, tb], [seg.abufs[m]], out=seg.a(m), in0=pu[:, :], in1=t[:, :], op=ALU.mult)
                for si, oseg in enumerate(segs[1:], start=1):
                    if si in lazy:
                        lazy[si]()
                    w = oseg.w
                    for (m, blks, _pg, _pgb, _pu, _pub) in grp:
                        pg, pgb = psum_next()
                        pu, pub = psum_next()
                        for kc in range(KC):
                            mm(pg[:, 0:w], blks[kc][0], oseg.nb(kc), kc == 0, kc == KC - 1, [blks[kc][1], oseg.nbbufs[kc]], [pgb])
                        for kc in range(KC):
                            mm(pu[:, 0:w], blks[16 + kc][0], oseg.nb(kc), kc == 0, kc == KC - 1,
                               [blks[16 + kc][1], oseg.nbbufs[kc]], [pub])
                        t, tb = tmp_next()
                        act(t[:, 0:w], pg[:, 0:w], AF.Silu, [pgb], [tb])
                        dve("tensor_tensor", [pub, tb], [oseg.abufs[m]], out=oseg.a(m), in0=pu[:, 0:w], in1=t[:, 0:w], op=ALU.mult)
                ws.release()
                for m in range(2):
                    if ("in", m) in hooks:
                        hooks[("in", m)]()
                m_start = 2
            for m in range(m_start, MC):
                blks = ws.take(32)
                for si, seg in enumerate(segs):
                    if m == 0 and si in lazy and not interleave:
                        lazy[si]()
                    w = seg.w
                    pg, pgb = psum_next()
                    pu, pub = psum_next()
                    for kc in range(KC):
                        mm(pg[:, 0:w], blks[kc][0], seg.nb(kc), kc == 0, kc == KC - 1, [blks[kc][1], seg.nbbufs[kc]], [pgb])
                    for kc in range(KC):
                        mm(pu[:, 0:w], blks[16 + kc][0], seg.nb(kc), kc == 0, kc == KC - 1,
                           [blks[16 + kc][1], seg.nbbufs[kc]], [pub])
                    t, tb = tmp_next()
                    act(t[:, 0:w], pg[:, 0:w], AF.Silu, [pgb], [tb])
                    dve("tensor_tensor", [pub, tb], [seg.abufs[m]], out=seg.a(m), in0=pu[:, 0:w], in1=t[:, 0:w], op=ALU.mult)
                ws.release()
                if ("in", m) in hooks:
                    hooks[("in", m)]()
            warm_sqrt()
            for f in range(KC):
                blks = ws.take(MC)
                for seg in segs:
                    w = seg.w
                    po, pob = psum_next()
                    for kc in range(MC):
                        mm(po[:, 0:w], blks[kc][0], seg.a(kc), kc == 0, kc == MC - 1, [blks[kc][1], seg.abufs[kc]], [pob])
                    dve("scalar_tensor_tensor", [pob, seg.hbufs[f]], [seg.hbufs[f]], out=seg.h(f), in0=po[:, 0:w], scalar=0.5,
                        in1=seg.h(f), op0=ALU.mult, op1=ALU.add)
                    if inline_next is not None and seg is segs[0]:
                        mid_step(seg, f)
                ws.release()
                if ("out", f) in hooks:
                    hooks[("out", f)]()
            if inline_next is not None:
                mid_finish(segs[0], inline_next)

        def vg(blk):
            return Rbf(4 * blk, 4)

        zbt = vg

        def mixer(tile, with_halo):
            seg_m = seg_ms[tile % 2]
            hT, hb = seg_m.hT, seg_m.hbufs
            if with_halo:
                rmsnorm(seg_h, V_MIX)
            for cg in range(4):
                wide = [ws.take_wide() for _ in range(KC)]
                pre = None
                if cg == 0:
                    pre = [psum_next() for _ in range(4)]
                    for kc in range(KC):
                        for blk in range(4):
                            mm(pre[blk][0][:, :], nbT[:, kc, blk * 128:(blk + 1) * 128], wide[kc][0], kc == 0, False,
                               [nbb[kc], wide[kc][1]], [pre[blk][1]])
                for blk in range(4):
                    if pre is not None:
                        ps, pb = pre[blk]
                    else:
                        ps, pb = psum_next()
                        for kc in range(KC):
                            mm(ps[:, :], nbT[:, kc, blk * 128:(blk + 1) * 128], wide[kc][0], kc == 0, False,
                               [nbb[kc], wide[kc][1]], [pb])
                    mm(ps[:, :], sel_bf[:, cg * 128:(cg + 1) * 128], brow_bf[:, :], False, True, [B["sel"], B["brow"]], [pb])
                    t, tb = tmp_next()
                    act(t[:, :], ps[:, :], AF.Gelu, [pb], [tb])
                    dve("bn_stats", [tb], [vstb[blk]], out=vstats[:, blk, cg, :], in_=t[:, :])
                    dve("tensor_copy", [tb], [Rb[4 * blk + cg]], out=vg(blk)[:, cg * 512:(cg + 1) * 512], in_=t[:, :])
                ws.release()
            for blk in range(4):
                dve("bn_aggr", [vstb[blk]], [mvb[blk]], out=mv[:, blk, :], in_=vstats[:, blk, :, :].rearrange("p a b -> p (a b)"))
                act(rstd[:, blk:blk + 1], mv[:, blk, 1:2], AF.Ln, [mvb[blk], B["cst"]], [mvb[blk]], bias=cst[:, 1:2])
                act(rstd[:, blk:blk + 1], rstd[:, blk:blk + 1], AF.Exp, [mvb[blk]], [mvb[blk]], scale=-0.5)
                dve("tensor_scalar", Rb[4 * blk:4 * blk + 4] + [mvb[blk]], Rb[4 * blk:4 * blk + 4], out=vg(blk), in0=vg(blk),
                    scalar1=mv[:, blk, 0:1], scalar2=rstd[:, blk:blk + 1], op0=ALU.subtract, op1=ALU.mult)
            pend = None
            for cc in range(KC + 1):
                cur = None
                if cc < KC:
                    blks = ws.take(16)
                    pu, pub = psum_next()
                    for kc in range(KC):
                        mm(pu[:, :], blks[kc][0], nbT[:, kc, :], kc == 0, kc == KC - 1, [blks[kc][1], nbb[kc]], [pub])
                    cur = (cc, pu, pub)
                if pend is not None:
                    c0, pu0, pub0 = pend
                    h = c0 // 2
                    pg, pgb = psum_next()
                    for blk in range(4):
                        mm(pg[:, blk * 128:(blk + 1) * 128], vg(blk)[:, c0 * 128:(c0 + 1) * 128], wm_bf[:, h, :], True, True,
                           [Rb[4 * blk + c0 // 4], B["wm"]], [pgb])
                    tu, tub = tmp_next()
                    act(tu[:, :], pu0[:, :], AF.Gelu, [pub0, B["vecs"]], [tub], bias=vecs[:, V_BU + c0:V_BU + c0 + 1])
                    tsn, tsb = tmp_next()
                    dve("scalar_tensor_tensor", [pgb, B["vecs"], B["beta"]], [tsb],
                        out=tsn[:, :].rearrange("p (a c) -> p a c", a=4), in0=pg[:, :].rearrange("p (a c) -> p a c", a=4),
                        scalar=vecs[:, V_LNG + c0:V_LNG + c0 + 1], in1=beta[:, c0, :].unsqueeze(1).broadcast_to([128, 4, 128]),
                        op0=ALU.mult, op1=ALU.add)
                    dve("tensor_tensor", [tub, tsb], [Rb[16 + c0]], out=Rbf(16 + c0), in0=tu[:, :], in1=tsn[:, :], op=ALU.mult)
                pend = cur
                ws.release()
            for cg in range(4):
                wide = [ws.take_wide() for _ in range(KC)]
                for blk in range(4):
                    ps, pb = psum_next()
                    for kc in range(KC):
                        mm(ps[:, :], nbT[:, kc, blk * 128:(blk + 1) * 128], wide[kc][0], kc == 0, False,
                           [nbb[kc], wide[kc][1]], [pb])
                    mm(ps[:, :], sel_bf[:, (4 + cg) * 128:(5 + cg) * 128], brow_bf[:, :], False, True, [B["sel"], B["brow"]], [pb])
                    if blk % 2 == 0:
                        act(zbt(blk)[:, cg * 512:(cg + 1) * 512], ps[:, :], AF.Copy, [pb], [Rb[4 * blk + cg]])
                    else:
                        dve("tensor_copy", [pb], [Rb[4 * blk + cg]], out=zbt(blk)[:, cg * 512:(cg + 1) * 512], in_=ps[:, :])
                if with_halo:
                    ps, pb = psum_next()
                    for kc in range(KC):
                        mm(ps[0:HW, :], nbT_h[:, kc, :], wide[kc][0], kc == 0, False, [nbb_h[kc], wide[kc][1]], [pb])
                    mm(ps[0:HW, :], sel_bf[:, (4 + cg) * 128:(4 + cg) * 128 + HW], brow_bf[:, :], False, True,
                       [B["sel"], B["brow"]], [pb])
                    act(zb_prev[0:HW, cg * 512:(cg + 1) * 512], ps[0:HW, :], AF.Copy, [pb], [B["zb_prev"]])
                ws.release()
            for cc in range(KC):
                g = cc // 4
                pp, ppb = psum_next()
                for blk in range(4):
                    cur = zbt(blk)[:, cc * 128:(cc + 1) * 128]
                    curb = Rb[4 * blk + g]
                    if blk == 0:
                        prv, prvb = zb_prev[:, cc * 128:(cc + 1) * 128], B["zb_prev"]
                    else:
                        prv, prvb = zbt(blk - 1)[:, cc * 128:(cc + 1) * 128], Rb[4 * (blk - 1) + g]
                    if tile == 0 and blk == 0:
                        ml = [(cur, curb, PThi[:, g, :], B["PThi"]), (cur, curb, PTlo[:, g, :], B["PTlo"]),
                              (prv, prvb, PThi[:, 4 + g, :], B["PThi"]), (prv, prvb, PTlo[:, 4 + g, :], B["PTlo"])]
                    else:
                        ml = [(cur, curb, PTstd[:, g, :], B["PTstd"]), (prv, prvb, PTstd[:, 4 + g, :], B["PTstd"])]
                    for i, (l, lb, r, rb) in enumerate(ml):
                        mm(pp[:, blk * 128:(blk + 1) * 128], l, r, i == 0, i == len(ml) - 1, [lb, rb], [ppb])
                if cc % 2 == 0:
                    act(Rbf(32 + cc), pp[:, :], AF.Copy, [ppb], [Rb[32 + cc]])
                else:
                    dve("tensor_copy", [ppb], [Rb[32 + cc]], out=Rbf(32 + cc), in_=pp[:, :])
            if tile < NT - 1:
                dve("tensor_copy", Rb[12:16], [B["zb_prev"]], out=zb_prev[:, :], in_=zbt(3))
            for dd in range(KC):
                g = dd // 4
                blks = ws.take(4)
                pm, pmb = psum_next()
                for kc in range(4):
                    mm(pm[:, :], blks[kc][0], Rbf(32 + g * 4 + kc), kc == 0, kc == 3, [blks[kc][1], Rb[32 + g * 4 + kc]], [pmb])
                act(Rbf(dd), pm[:, :], AF.Copy, [pmb, B["vecs"]], [Rb[dd]], scale=vecs[:, V_PSC + dd:V_PSC + dd + 1])
                ws.release()
            for f in range(KC):
                blks = ws.take(32)
                pga, pgab = psum_next()
                for kc in range(KC):
                    mm(pga[:, :], blks[kc][0], nbT[:, kc, :], kc == 0, kc == KC - 1, [blks[kc][1], nbb[kc]], [pgab])
                pya, pyab = psum_next()
                for kc in range(KC):
                    mm(pya[:, :], blks[16 + kc][0], Rbf(16 + kc), kc == 0, kc == KC - 1, [blks[16 + kc][1], Rb[16 + kc]], [pyab])
                blks = ws.take(32)
                pgb_, pgbb = psum_next()
                for kc in range(KC):
                    mm(pgb_[:, :], blks[kc][0], nbT[:, kc, :], kc == 0, kc == KC - 1, [blks[kc][1], nbb[kc]], [pgbb])
                pyb, pybb = psum_next()
                for kc in range(KC):
                    mm(pyb[:, :], blks[16 + kc][0], Rbf(kc), kc == 0, kc == KC - 1, [blks[16 + kc][1], Rb[kc]], [pybb])
                ws.release()
                t1, t1b = tmp_next()
                act(t1[:, :], pga[:, :], AF.Sigmoid, [pgab, B["vecs"]], [t1b], bias=vecs[:, V_BGA + f:V_BGA + f + 1])
                dve("tensor_tensor", [t1b, pyab], [t1b], out=t1[:, :], in0=pya[:, :], in1=t1[:, :], op=ALU.mult)
                t2, t2b = tmp_next()
                act(t2[:, :], pgb_[:, :], AF.Sigmoid, [pgbb, B["vecs"]], [t2b], bias=vecs[:, V_BGB + f:V_BGB + f + 1])
                dve("tensor_tensor", [t2b, pybb], [t2b], out=t2[:, :], in0=pyb[:, :], in1=t2[:, :], op=ALU.mult)
                dve("tensor_tensor", [t1b, t2b], [Rb[32 + f]], out=Rbf(32 + f), in0=t1[:, :], in1=t2[:, :], op=ALU.add)
            warm_sqrt()
            for f in range(KC):
                blks = ws.take(16)
                po, pob = psum_next()
                for kc in range(KC):
                    mm(po[:, :], blks[kc][0], Rbf(32 + kc), kc == 0, kc == KC - 1, [blks[kc][1], Rb[32 + kc]], [pob])
                dve("tensor_tensor", [pob, hb[f]], [hb[f]], out=hT[:, f, :], in0=po[:, :], in1=hT[:, f, :], op=ALU.add)
                mid_step(seg_m, f)
                ws.release()
            mid_finish(seg_m, V_FFN2)

        stores = []

        def load_x(tile):
            p = tile % 2
            dma("sp", hTs[p][:, :, :], xT[:, tile * T:(tile + 1) * T].rearrange("(k p) n -> p k n", p=128), xsems[p], [], hbs[p])

        def prenorm(tile):
            rmsnorm(seg_ms[tile % 2], V_FFN1)

        fin = {}

        def fin_sq(tile, kc):
            seg = seg_ms[tile % 2]
            u = 44 + kc % 4
            act(Rbf(u), seg.h(kc), AF.Square, [seg.hbufs[kc]], [Rb[u]])

        def fin_red(tile, kc):
            if kc == 0:
                fin["bank"] = psum_reserve()
            i = fin["bank"]
            u = 44 + kc % 4
            mm(pss[i][:, :], ones_bf[:, :], Rbf(u), kc == 0, kc == KC - 1, [B["ones_bf"], Rb[u]], [psb[i]])

        def fin_squares(tile, bt):
            for kc in range(4 * bt, 4 * bt + 4):
                fin_sq(tile, kc)

        def fin_reduce(tile, bt):
            for kc in range(4 * bt, 4 * bt + 4):
                fin_red(tile, kc)

        PIECE_END = {3: (0, 0), 7: (1, 4), 11: (2, 8), 13: (3, 12), 15: (4, 14)}

        def fin_finish(tile, split=False):
            p = tile % 2
            seg = seg_ms[p]
            i = fin["bank"]
            act(rs_f[:, :], pss[i][:, :], AF.Ln, [psb[i], B["cst"]], [B["rs_f"]], scale=1.0 / D, bias=cst[:, 0:1])
            reserved.discard(i)
            act(rs_f[:, :], rs_f[:, :], AF.Exp, [B["rs_f"]], [B["rs_f"]], scale=-0.5)
            ov = outT[:, tile * T:(tile + 1) * T].rearrange("(k p) n -> p k n", p=128)
            for kc in range(KC):
                dve("scalar_tensor_tensor", [seg.hbufs[kc], B["rs_f"], B["vecs"]], [seg.hbufs[kc]], out=seg.h(kc),
                    in0=seg.h(kc), scalar=vecs[:, V_FIN + kc:V_FIN + kc + 1], in1=rs_f[:, :], op0=ALU.mult, op1=ALU.mult)
                if split and kc in PIECE_END:
                    q, c0 = PIECE_END[kc]
                    stores.append(dma("sp", ov[:, c0:kc + 1, :], hTs[p][:, c0:kc + 1, :],
                                      osems[p] if q == 0 else lsems[q - 1], seg.hbufs[c0:kc + 1], []))
            if not split:
                stores.append(dma("sp", ov, hTs[p][:, :, :], osems[p], seg.hbufs, []))

        def fin_hooks(tile, hooks, key, first, after=None):
            def mk(j):
                def f():
                    if j >= 1:
                        fin_reduce(tile, j - 1)
                    if j <= 3:
                        fin_squares(tile, j)
                    if j == 4:
                        fin_finish(tile)
                        if after is not None:
                            after()
                return f
            for j in range(5):
                hooks[(key, first + j)] = mk(j)

        warm_sqrt()
        prenorm(0)
        for tile in range(NT):
            first = tile == 0
            last = tile == NT - 1
            seg_m = seg_ms[tile % 2]
            hooks1 = {}
            if first:
                hooks1[("in", 1)] = setup_part2

                def h0():
                    setup_part3()
                    load_x(1)
                hooks1[("in", 3)] = h0
            else:
                fin_hooks(tile - 1, hooks1, "in", 1,
                          after=(lambda tile=tile: load_x(tile + 1)) if tile + 1 < NT else None)
            ffn([seg_m, seg_h] if first else [seg_m], V_FFN1, do_norm=False, hooks=hooks1,
                lazy={1: lambda: rmsnorm(seg_h, V_FFN1)} if first else None, inline_next=V_MIX, interleave=first)
            mixer(tile, first)
            hooks2 = {}
            if not last:
                nxt = seg_ms[(tile + 1) % 2]
                hooks2[("out", 1)] = lambda nxt=nxt: rms_squares(nxt)

                def h3(nxt=nxt):
                    rms_reduce(nxt)
                    rms_apply(nxt, V_FFN1)
                hooks2[("out", 3)] = h3
            else:
                for f in range(KC):
                    def hf(f=f, tile=tile):
                        if f >= 1:
                            fin_red(tile, f - 1)
                        fin_sq(tile, f)
                    hooks2[("out", f)] = hf
            ffn([seg_m], V_FFN2, do_norm=False, hooks=hooks2, interleave=True)
        fin_red(NT - 1, KC - 1)
        fin_finish(NT - 1, split=True)
        assert ws.pos == ITEMS_PER_PASS * SLOT_BLKS * NT, ws.pos
        P.add("sp", None, extra_deps=stores)
        P.emit(nc, block, sems)
    return nc


def _blocks(w):
    k, n = w.shape
    return w.reshape(k // 128, 128, n // 128, 128).transpose(0, 2, 1, 3)


def _ffn_stream(w_in, w_out):
    b = _blocks(w_in)
    s1 = np.stack([b[:, :MC], b[:, MC:]], axis=0).transpose(2, 0, 1, 3, 4).reshape(-1, 128, 128)
    s2 = _blocks(w_out).transpose(1, 0, 2, 3).reshape(-1, 128, 128)
    return [s1, s2]


def _build_wstream(inp):
    w_in = inp["w_in"]
    parts = _ffn_stream(inp["ffn1_w_in"], inp["ffn1_w_out"])
    bv = _blocks(w_in[:, 2048:4096])
    parts.append(bv.reshape(16, 4, 4, 128, 128).transpose(1, 0, 2, 3, 4).reshape(-1, 128, 128))
    parts.append(_blocks(w_in[:, 0:2048]).transpose(1, 0, 2, 3).reshape(-1, 128, 128))
    bz = _blocks(w_in[:, 4096:6144])
    parts.append(bz.reshape(16, 4, 4, 128, 128).transpose(1, 0, 2, 3, 4).reshape(-1, 128, 128))
    pw = inp["pool_w"].reshape(4, 4, 128, 4, 128).transpose(0, 3, 1, 2, 4)
    parts.append(pw.reshape(-1, 128, 128))
    ga = _blocks(w_in[:, 6144:8192]).transpose(1, 0, 2, 3)
    gb = _blocks(w_in[:, 8192:10240]).transpose(1, 0, 2, 3)
    ya = _blocks(inp["w_branch_a"]).transpose(1, 0, 2, 3)
    yb = _blocks(inp["w_branch_b"]).transpose(1, 0, 2, 3)
    parts.append(np.stack([ga, ya, gb, yb], axis=1).reshape(-1, 128, 128))
    parts.append(_blocks(inp["w_out"]).transpose(1, 0, 2, 3).reshape(-1, 128, 128))
    parts += _ffn_stream(inp["ffn2_w_in"], inp["ffn2_w_out"])
    allb = np.concatenate(parts, axis=0)
    assert allb.shape[0] == ITEMS_PER_PASS * SLOT_BLKS, allb.shape
    return np.ascontiguousarray(
        allb.reshape(ITEMS_PER_PASS, SLOT_BLKS, 128, 128).transpose(0, 2, 1, 3).reshape(ITEMS_PER_PASS, 128, SLOT_BLKS * 128))


def _pool_mats(core):
    pm = np.zeros((128, 16, 128), np.float32)
    t = np.arange(128)
    for g, w in enumerate(POOL_WINDOWS):
        for kind in range(4):
            m = np.zeros((128, 128), np.float32)
            for tt in t:
                gpos = tt if core == 0 else tt + 1000000
                cnt = min(gpos + 1, w) if kind >= 2 else w
                for dj in range(w):
                    src = tt - dj
                    if kind in (0, 2):
                        if src >= 0:
                            m[src, tt] += 1.0 / cnt
                    elif kind == 1:
                        if src < 0:
                            m[128 + src, tt] += 1.0 / cnt
                    else:
                        if src < 0 and core > 0:
                            m[HW + src, tt] += 1.0 / cnt
                if kind in (0, 2):
                    m[tt, tt] -= 1.0
            pm[:, kind * 4 + g, :] = m
    return pm.reshape(128, 16 * 128)


def _feat(v):
    return np.asarray(v, np.float32).reshape(16, 128).T


def kernel(**inp):
    inp = {k: np.asarray(v) for k, v in inp.items()}
    x = inp["x"].reshape(SEQ, D).astype(np.float32, copy=False)
    wstream = _build_wstream(inp)
    b_in = inp["b_in"].astype(np.float32)
    vecs = np.ascontiguousarray(np.concatenate([
        _feat(inp["ffn1_norm"]), _feat(inp["mix_norm"]), _feat(inp["ffn2_norm"]), _feat(inp["final_norm"]),
        _feat(b_in[0:2048]), _feat(b_in[6144:8192]), _feat(b_in[8192:10240]),
        _feat(inp["sgu_ln_g"]), _feat(inp["sgu_ln_b"]), _feat(inp["pool_scale"])], axis=1))
    brow8 = np.concatenate([b_in[2048:4096].reshape(4, 512), b_in[4096:6144].reshape(4, 512)], axis=0)
    brow = np.ascontiguousarray(np.concatenate([brow8, brow8], axis=0))
    sel = np.zeros((16, 8, 128), np.float32)
    for q in range(8):
        sel[q, q, :] = 1.0
        sel[8 + q, q, :] = 1.0
    sel = sel.reshape(16, 8 * 128)
    mask16 = np.zeros((16, 2), np.float32)
    mask16[:8, 0] = 1.0
    mask16[8:, 1] = 1.0
    wsT = np.ascontiguousarray(inp["sgu_w_s"].astype(np.float32).transpose(2, 0, 1)).reshape(128, 8 * 128)
    bsrow = np.ascontiguousarray(inp["sgu_b_s"].astype(np.float32).reshape(8 * 128))
    nc = build_program()
    in_maps = []
    for c in range(NCORES):
        xs = x[c * TOK:(c + 1) * TOK]
        if c == 0:
            halo = np.zeros((HW, D), np.float32)
        else:
            halo = x[c * TOK - HW:c * TOK]
        in_maps.append({
            "xT": np.ascontiguousarray(xs.T), "xh": np.ascontiguousarray(halo.T), "wstream": wstream, "vecs": vecs,
            "brow": brow, "sel": sel, "mask16": mask16, "wsT": wsT, "bsrow": bsrow, "pmats": _pool_mats(c),
        })
    res = run_bass_kernel_spmd(nc, in_maps, core_ids=list(range(NCORES)))
    out = np.empty((SEQ, D), np.float32)
    for c in range(NCORES):
        out[c * TOK:(c + 1) * TOK] = res.results[c]["outT"].T
    return out.reshape(1, SEQ, D)
```

```python
import numpy as np
import concourse.bass as bass
import concourse.mybir as mybir
from concourse.bass_utils import run_bass_kernel_spmd

F32 = mybir.dt.float32
BF16 = mybir.dt.bfloat16
AF = mybir.ActivationFunctionType
ALU = mybir.AluOpType

COMPUTE = ("pe", "act", "dve")
QUEUES = ("sp", "pool")


class Buf:
    __slots__ = ("name", "writer", "readers")

    def __init__(self, name):
        self.name = name
        self.writer = None
        self.readers = {}


class Op:
    __slots__ = ("eng", "fn", "deps", "signal", "count", "sem", "val")

    def __init__(self, eng, fn):
        self.eng = eng
        self.fn = fn
        self.deps = []
        self.signal = False
        self.count = 0
        self.sem = None
        self.val = 0


class Prog:
    def __init__(self):
        self.ops = {e: [] for e in COMPUTE + QUEUES}
        self.dma_counts = {}

    def add(self, eng, fn, reads=(), writes=(), dma_sem=None, extra_deps=()):
        op = Op(eng, fn)
        is_dma = dma_sem is not None
        deps = []
        for b in reads:
            if b.writer is not None:
                deps.append((b.writer, True))
        for b in writes:
            if b.writer is not None:
                deps.append((b.writer, False))
            for r in b.readers.values():
                deps.append((r, False))
        for d in extra_deps:
            deps.append((d, True))
        seen = set()
        for d, raw in deps:
            if d is op or id(d) in seen:
                continue
            if d.sem is None and d.eng == eng and not is_dma:
                if eng == "pe":
                    continue
            seen.add(id(d))
            op.deps.append(d)
            if d.sem is None:
                d.signal = True
        if is_dma:
            n = self.dma_counts.get(id(dma_sem), 0) + 1
            self.dma_counts[id(dma_sem)] = n
            op.sem = dma_sem
            op.val = 16 * n
        for b in reads:
            b.readers[eng if not is_dma else ("dma", id(op))] = op
        for b in writes:
            b.writer = op
            b.readers = {}
        self.ops[eng].append(op)
        return op

    def emit(self, nc, block, sems):
        for e in COMPUTE:
            c = 0
            for op in self.ops[e]:
                if op.signal:
                    c += 1
                op.count = c
        prog = self

        def run(engname, eng):
            waited = {}
            embed = engname in COMPUTE
            for op in prog.ops[engname]:
                need = {}
                for d in op.deps:
                    if d.sem is not None:
                        s, v = d.sem, d.val
                    else:
                        s, v = sems[d.eng], d.count
                    if waited.get(id(s), 0) < v:
                        waited[id(s)] = v
                        need[id(s)] = (s, v)
                need = list(need.values())
                last = need.pop() if (need and embed and op.fn is not None) else None
                for s, v in need:
                    eng.wait_ge(s, v)
                if op.fn is None:
                    continue
                ins = op.fn(eng)
                if last is not None:
                    ins._wait_ge(last[0], last[1])
                if op.sem is not None:
                    ins.then_inc(op.sem, 16)
                elif op.signal:
                    ins.then_inc(sems[engname], 1)

        @block.tensor
        def _(t):
            run("pe", t)

        @block.scalar
        def _(a):
            run("act", a)

        @block.vector
        def _(v):
            run("dve", v)

        @block.sync
        def _(s):
            run("sp", s)

        @block.gpsimd
        def _(g):
            run("pool", g)


NCORES = 8
D = 2048
SEQ = 16384
TOK = SEQ // NCORES
T = 512
NT = TOK // T
HW = 16
DFF = 5632
KC = D // 128
MC = DFF // 128
NSLOT = 4
SLOT_BLKS = 32
ITEMS_PER_PASS = 198
RMS_EPS = 1e-6
LN_EPS = 1e-5
POOL_WINDOWS = (2, 4, 8, 16)

V_FFN1, V_MIX, V_FFN2, V_FIN, V_BU, V_BGA, V_BGB, V_LNG, V_LNB, V_PSC = [16 * i for i in range(10)]
NV = 160


class Seg:
    def __init__(self, w, hT, nbT, a_fn, hbufs, nbbufs, abufs, rs, rsbuf):
        self.w = w
        self.hT = hT
        self.nbT = nbT
        self.a_fn = a_fn
        self.hbufs = hbufs
        self.nbbufs = nbbufs
        self.abufs = abufs
        self.rs = rs
        self.rsbuf = rsbuf

    def h(self, kc):
        return self.hT[:, kc, 0:self.w]

    def nb(self, kc):
        return self.nbT[:, kc, 0:self.w]

    def a(self, m):
        return self.a_fn(m)


def build_program():
    nc = bass.Bass("TRN2", target_bir_lowering=False)
    P = Prog()
    xT = nc.dram_tensor("xT", [D, TOK], F32, kind="ExternalInput").ap()
    xh = nc.dram_tensor("xh", [D, HW], F32, kind="ExternalInput").ap()
    wst = nc.dram_tensor("wstream", [ITEMS_PER_PASS, 128, SLOT_BLKS * 128], F32, kind="ExternalInput").ap()
    vecs_d = nc.dram_tensor("vecs", [128, NV], F32, kind="ExternalInput").ap()
    brow_d = nc.dram_tensor("brow", [16, 512], F32, kind="ExternalInput").ap()
    sel_d = nc.dram_tensor("sel", [16, 8 * 128], F32, kind="ExternalInput").ap()
    mask_d = nc.dram_tensor("mask16", [16, 2], F32, kind="ExternalInput").ap()
    wsT_d = nc.dram_tensor("wsT", [128, 8 * 128], F32, kind="ExternalInput").ap()
    bs_d = nc.dram_tensor("bsrow", [8 * 128], F32, kind="ExternalInput").ap()
    pm_d = nc.dram_tensor("pmats", [128, 16 * 128], F32, kind="ExternalInput").ap()
    outT = nc.dram_tensor("outT", [D, TOK], F32, kind="ExternalOutput").ap()

    A = nc.alloc_sbuf_tensor
    hTs = [A("hT0", [128, KC, T], F32), A("hT1", [128, KC, T], F32)]
    nbT = A("nbT", [128, KC, T], BF16)
    R = A("R", [128, 48 * 256], F32)
    slots = [A(f"slot{i}", [128, SLOT_BLKS * 128], BF16) for i in range(NSLOT)]
    hT_h = A("hT_h", [128, KC, HW], F32)
    nbT_h = A("nbT_h", [128, KC, HW], BF16)
    aT_h = A("aT_h", [128, MC, HW], BF16)
    NTMP = 4
    tmps = [A(f"tmp{i}", [128, T], F32) for i in range(NTMP)]
    rs_m = A("rs_m", [128, T], F32)
    rs_h = A("rs_h", [128, HW], F32)
    rs_f = A("rs_f", [128, T], F32)
    beta = A("beta", [128, KC, 128], F32)
    wm_bf = A("wm_bf", [128, 8, 128], BF16)
    PTstd = A("PTstd", [128, 8, 128], BF16)
    PThi = A("PThi", [128, 8, 128], BF16)
    PTlo = A("PTlo", [128, 8, 128], BF16)
    ones_bf = A("ones_bf", [128, 128], BF16)
    ones_f = A("ones_f", [128, 128], F32)
    vecs = A("vecs_sb", [128, NV], F32)
    cst = A("cst", [128, 2], F32)
    brow_bf = A("brow_bf", [128, 512], BF16)
    sel_bf = A("sel_bf", [128, 8 * 128], BF16)
    mask_sb = A("mask_sb", [16, 2], F32)
    zb_prev = A("zb_prev", [128, D], BF16)
    vstats = A("vstats", [128, 4, 4, 6], F32)
    mv = A("mv", [128, 4, 2], F32)
    rstd = A("rstd", [128, 4], F32)
    warm = A("warm", [128, 2], F32)

    pss = [nc.alloc_psum_tensor(f"ps{i}", [128, T], F32) for i in range(8)]

    hbs = [[Buf(f"h{p}_{k}") for k in range(KC)] for p in range(2)]
    nbb = [Buf(f"nb{k}") for k in range(KC)]
    Rb = [Buf(f"R{u}") for u in range(48)]
    slotb = [Buf(f"slot{i}") for i in range(NSLOT)]
    hb_h = [Buf(f"hh{k}") for k in range(KC)]
    nbb_h = [Buf(f"nbh{k}") for k in range(KC)]
    ab_h = [Buf(f"ah{m}") for m in range(MC)]
    tmpb = [Buf(f"tmp{i}") for i in range(NTMP)]
    psb = [Buf(f"ps{i}") for i in range(8)]
    B = {k: Buf(k) for k in ["rs_m", "rs_h", "rs_f", "beta", "wm", "PTstd", "PThi", "PTlo", "ones_bf", "ones_f", "vecs",
                              "cst", "brow", "sel", "mask", "zb_prev"]}
    vstb = [Buf(f"vst{i}") for i in range(4)]
    warmb = Buf("warm")
    mvb = [Buf(f"mv{i}") for i in range(4)]

    def Rbf(u0, n=1):
        return R[:, u0 * 256:(u0 + n) * 256].bitcast(BF16)

    def Rf32(u0, n=1):
        return R[:, u0 * 256:(u0 + n) * 256]

    seg_ms = [Seg(T, hTs[p], nbT, lambda m: Rbf(m), hbs[p], nbb, Rb, rs_m, B["rs_m"]) for p in range(2)]
    seg_h = Seg(HW, hT_h, nbT_h, lambda m: aT_h[:, m, :], hb_h, nbb_h, ab_h, rs_h, B["rs_h"])

    from contextlib import ExitStack
    with ExitStack() as es:
        sems = {e: es.enter_context(nc.semaphore(f"s_{e}")) for e in COMPUTE}
        wsems = [es.enter_context(nc.semaphore(f"s_w{i}")) for i in range(NSLOT)]
        xsems = [es.enter_context(nc.semaphore(f"s_x{i}")) for i in range(2)]
        osems = [es.enter_context(nc.semaphore(f"s_o{i}")) for i in range(2)]
        csems = [es.enter_context(nc.semaphore(f"s_c{i}")) for i in range(9)]
        x0sems = [es.enter_context(nc.semaphore(f"s_x0_{i}")) for i in range(3)]
        lsems = [es.enter_context(nc.semaphore(f"s_l{i}")) for i in range(4)]
        cctr = [0]

        def csem_next():
            cctr[0] += 1
            return csems[cctr[0] - 1]
        block = es.enter_context(nc.Block())

        def mm(out, lhsT, rhs, start, stop, reads, writes):
            return P.add("pe", lambda e: e.matmul(out, lhsT, rhs, start=start, stop=stop), reads=reads, writes=writes)

        def act(out, in_, func, reads, writes, **kw):
            return P.add("act", lambda e: e.activation(out=out, in_=in_, func=func, **kw), reads=reads, writes=writes)

        def dve(method, reads, writes, **kw):
            return P.add("dve", lambda e: getattr(e, method)(**kw), reads=reads, writes=writes)

        def dma(q, out, in_, sem, reads, writes, extra=()):
            return P.add(q, lambda e: e.dma_start(out=out, in_=in_), reads=reads, writes=writes, dma_sem=sem,
                         extra_deps=extra)

        pctr = [0]

        reserved = set()

        def psum_next():
            while True:
                i = pctr[0] % 8
                pctr[0] += 1
                if i not in reserved:
                    return pss[i], psb[i]

        def psum_reserve():
            while True:
                i = pctr[0] % 8
                pctr[0] += 1
                if i not in reserved:
                    reserved.add(i)
                    return i

        tctr = [0]

        def tmp_next():
            i = tctr[0] % NTMP
            tctr[0] += 1
            return tmps[i], tmpb[i]

        class WS:
            def __init__(self):
                self.total = ITEMS_PER_PASS * NT
                self.issued = 0
                self.pos = 0

            def issue(self):
                g = self.issued
                if g >= self.total:
                    return
                s = g % NSLOT
                dma("pool", slots[s][:, :], wst[g % ITEMS_PER_PASS], wsems[s], [], [slotb[s]],
                    extra=x0_ops[:2] if g == 0 else (x0_ops if g == 1 else ()))
                self.issued += 1

            def _loc(self):
                g, b = divmod(self.pos, SLOT_BLKS)
                assert g < self.issued, (g, self.issued)
                return g % NSLOT, b

            def take(self, n):
                out = []
                for _ in range(n):
                    s, b = self._loc()
                    out.append((slots[s][:, b * 128:(b + 1) * 128], slotb[s]))
                    self.pos += 1
                return out

            def take_wide(self):
                s, b = self._loc()
                assert b % 4 == 0
                self.pos += 4
                return slots[s][:, b * 128:(b + 4) * 128], slotb[s]

            def release(self):
                done = self.pos // SLOT_BLKS
                while self.issued < min(done + NSLOT, self.total):
                    self.issue()

        ws = WS()
        x0_ops = []

        x0v = xT[:, 0:T].rearrange("(k p) n -> p k n", p=128)
        for q in range(4):
            x0_ops.append(dma("sp", hTs[0][:, 4 * q:4 * q + 4, :], x0v[:, 4 * q:4 * q + 4, :],
                              xsems[0] if q == 0 else x0sems[q - 1], [], hbs[0][4 * q:4 * q + 4]))
        for _ in range(NSLOT):
            ws.issue()
        dma("sp", vecs[:, :], vecs_d[:, :], csem_next(), [], [B["vecs"]])
        dma("sp", hT_h[:, :, :], xh.rearrange("(k p) n -> p k n", p=128), csem_next(), [], hb_h)
        dma("sp", mask_sb[:, :], mask_d[:, :], csem_next(), [], [B["mask"]])
        SB = hbs[1]

        def stg(k0, n):
            return hTs[1][:, k0:k0 + n, :].rearrange("p a b -> p (a b)")

        pm32 = stg(0, 4).rearrange("p (k n) -> p k n", k=16)
        ws32 = stg(4, 2).rearrange("p (h n) -> p h n", h=8)
        bs32 = stg(6, 2).rearrange("p (h n) -> p h n", h=8)
        br32 = stg(8, 1)[0:16, :]
        brhi = stg(9, 1)[0:16, :]
        brd = stg(10, 1)[0:16, :]
        sel32 = stg(11, 2)[0:16, :]
        pmhi32 = stg(13, 2).rearrange("p (k n) -> p k n", k=8)
        dma("sp", stg(0, 4), pm_d[:, :], csem_next(), [], SB[0:4])
        dma("sp", stg(4, 2), wsT_d[:, :], csem_next(), [], SB[4:6])
        dma("sp", stg(6, 2), bs_d.partition_broadcast(128), csem_next(), [], SB[6:8])
        dma("sp", br32, brow_d[:, :], csem_next(), [], SB[8:9])
        dma("sp", sel32, sel_d[:, :], csem_next(), [], SB[11:13])

        dve("memset", [], [B["ones_bf"]], ap=ones_bf[:, :], constant=1.0)
        dve("memset", [], [B["cst"]], ap=cst[:, 0:1], constant=RMS_EPS)
        dve("memset", [], [B["cst"]], ap=cst[:, 1:2], constant=LN_EPS)
        dve("memset", [], [warmb], ap=warm[:, :], constant=1.0)

        def setup_part2():
            dve("memset", [], [B["ones_f"]], ap=ones_f[:, :], constant=1.0)
            dve("memset", [], [B["zb_prev"]], ap=zb_prev[:, :], constant=0.0)
            dve("tensor_copy", SB[0:4], [B["PTstd"]], out=PTstd[:, :, :], in_=pm32[:, 0:8, :])
            dve("tensor_copy", SB[0:4], [B["PThi"]], out=PThi[:, :, :], in_=pm32[:, 8:16, :])
            dve("tensor_copy", [B["PThi"]], SB[13:15], out=pmhi32, in_=PThi[:, :, :])
            dve("tensor_tensor", SB[0:4] + SB[13:15], [B["PTlo"]], out=PTlo[:, :, :], in0=pm32[:, 8:16, :], in1=pmhi32,
                op=ALU.subtract)
            dve("memset", [], SB[4:6], ap=ws32[64:128, :, 0:64], constant=0.0)
            dve("tensor_copy", SB[4:6], [B["wm"]], out=wm_bf[:, :, :], in_=ws32)
            dve("memset", [], [B["brow"]], ap=brow_bf[:, :], constant=0.0)
            dve("memset", [], [B["sel"]], ap=sel_bf[:, :], constant=0.0)
            dve("tensor_copy", SB[8:9], [B["brow"]], out=brow_bf[0:16, :], in_=br32)
            dve("tensor_copy", [B["brow"]], SB[9:10], out=brhi, in_=brow_bf[0:16, :])
            dve("tensor_tensor", SB[8:10], SB[10:11], out=brd, in0=br32, in1=brhi, op=ALU.subtract)
            dve("tensor_scalar", SB[9:10] + [B["mask"]], SB[9:10], out=brhi, in0=brhi, scalar1=mask_sb[:, 0:1], scalar2=None,
                op0=ALU.mult)
            dve("scalar_tensor_tensor", SB[9:11] + [B["mask"]], SB[10:11], out=brd, in0=brd, scalar=mask_sb[:, 1:2], in1=brhi,
                op0=ALU.mult, op1=ALU.add)
            dve("tensor_copy", SB[10:11], [B["brow"]], out=brow_bf[0:16, :], in_=brd)
            dve("tensor_copy", SB[11:13], [B["sel"]], out=sel_bf[0:16, :], in_=sel32)

        def setup_part3():
            for half in range(2):
                ps, pb = psum_next()
                for hh in range(4):
                    h = half * 4 + hh
                    mm(ps[:, hh * 128:(hh + 1) * 128], ones_f[:, :], ws32[:, h, :], True, True, [B["ones_f"]] + SB[4:6], [pb])
                for hh in range(4):
                    h = half * 4 + hh
                    for c2 in range(2):
                        cc = h * 2 + c2
                        dve("scalar_tensor_tensor", [pb, B["vecs"]] + SB[6:8], [B["beta"]], out=beta[:, cc, :],
                            in0=ps[:, hh * 128:(hh + 1) * 128], scalar=vecs[:, V_LNB + cc:V_LNB + cc + 1], in1=bs32[:, h, :],
                            op0=ALU.mult, op1=ALU.add)

        def rms_squares(seg):
            for kc in range(KC):
                act(seg.nb(kc), seg.h(kc), AF.Square, [seg.hbufs[kc]], [seg.nbbufs[kc]])

        def rms_reduce(seg):
            w = seg.w
            ps, pb = psum_next()
            for kc in range(KC):
                mm(ps[:, 0:w], ones_bf[:, :], seg.nb(kc), kc == 0, kc == KC - 1, [B["ones_bf"], seg.nbbufs[kc]], [pb])
            act(seg.rs[:, 0:w], ps[:, 0:w], AF.Ln, [pb, B["cst"]], [seg.rsbuf], scale=1.0 / D, bias=cst[:, 0:1])
            act(seg.rs[:, 0:w], seg.rs[:, 0:w], AF.Exp, [seg.rsbuf], [seg.rsbuf], scale=-0.5)

        def rms_apply(seg, vcol):
            w = seg.w
            for kc in range(KC):
                dve("scalar_tensor_tensor", [seg.hbufs[kc], seg.rsbuf, B["vecs"]], [seg.nbbufs[kc]], out=seg.nb(kc),
                    in0=seg.h(kc), scalar=vecs[:, vcol + kc:vcol + kc + 1], in1=seg.rs[:, 0:w], op0=ALU.mult, op1=ALU.mult)

        def warm_sqrt():
            act(warm[:, 1:2], warm[:, 0:1], AF.Ln, [warmb], [warmb])

        mid = {}

        def mid_sq(seg, kc):
            act(seg.nb(kc), seg.h(kc), AF.Square, [seg.hbufs[kc]], [seg.nbbufs[kc]])

        def mid_red(seg, kc):
            if kc == 0:
                mid["bank"] = psum_reserve()
            i = mid["bank"]
            mm(pss[i][:, 0:seg.w], ones_bf[:, :], seg.nb(kc), kc == 0, kc == KC - 1, [B["ones_bf"], seg.nbbufs[kc]], [psb[i]])

        def mid_step(seg, f):
            mid_sq(seg, f)
            if f >= 1:
                mid_red(seg, f - 1)

        def mid_finish(seg, vcol):
            w = seg.w
            mid_red(seg, KC - 1)
            i = mid["bank"]
            act(seg.rs[:, 0:w], pss[i][:, 0:w], AF.Ln, [psb[i], B["cst"]], [seg.rsbuf], scale=1.0 / D, bias=cst[:, 0:1])
            reserved.discard(i)
            act(seg.rs[:, 0:w], seg.rs[:, 0:w], AF.Exp, [seg.rsbuf], [seg.rsbuf], scale=-0.5)
            rms_apply(seg, vcol)

        def rmsnorm(seg, vcol):
            rms_squares(seg)
            rms_reduce(seg)
            rms_apply(seg, vcol)

        def ffn(segs, vcol, do_norm=True, hooks=None, lazy=None, inline_next=None, interleave=False):
            hooks = hooks or {}
            lazy = lazy or {}
            if do_norm:
                for seg in segs:
                    rmsnorm(seg, vcol)
            m_start = 0
            if interleave:
                seg = segs[0]
                grp = []
                for m in range(2):
                    blks = ws.take(32)
                    pg, pgb = psum_next()
                    pu, pub = psum_next()
                    grp.append((m, blks, pg, pgb, pu, pub))
                for kc in range(KC):
                    for (m, blks, pg, pgb, pu, pub) in grp:
                        mm(pg[:, :], blks[kc][0], seg.nb(kc), kc == 0, kc == KC - 1, [blks[kc][1], seg.nbbufs[kc]], [pgb])
                        mm(pu[:, :], blks[16 + kc][0], seg.nb(kc), kc == 0, kc == KC - 1,
                           [blks[16 + kc][1], seg.nbbufs[kc]], [pub])
                for (m, blks, pg, pgb, pu, pub) in grp:
                    t, tb = tmp_next()
                    act(t[:, :], pg[:, :], AF.Silu, [pgb], [tb])
                    dve("tensor_tensor", [pub, tb], [seg.abufs[m]], out=seg.a(m), in0=pu[:, :], in1=t[:, :], op=ALU.mult)
                for si, oseg in enumerate(segs[1:], start=1):
                    if si in lazy:
                        lazy[si]()
                    w = oseg.w
                    for (m, blks, _pg, _pgb, _pu, _pub) in grp:
                        pg, pgb = psum_next()
                        pu, pub = psum_next()
                        for kc in range(KC):
                            mm(pg[:, 0:w], blks[kc][0], oseg.nb(kc), kc == 0, kc == KC - 1, [blks[kc][1], oseg.nbbufs[kc]], [pgb])
                        for kc in range(KC):
                            mm(pu[:, 0:w], blks[16 + kc][0], oseg.nb(kc), kc == 0, kc == KC - 1,
                               [blks[16 + kc][1], oseg.nbbufs[kc]], [pub])
                        t, tb = tmp_next()
                        act(t[:, 0:w], pg[:, 0:w], AF.Silu, [pgb], [tb])
                        dve("tensor_tensor", [pub, tb], [oseg.abufs[m]], out=oseg.a(m), in0=pu[:, 0:w], in1=t[:, 0:w], op=ALU.mult)
                ws.release()
                for m in range(2):
                    if ("in", m) in hooks:
                        hooks[("in", m)]()
                m_start = 2
            for m in range(m_start, MC):
                blks = ws.take(32)
                for si, seg in enumerate(segs):
                    if m == 0 and si in lazy and not interleave:
                        lazy[si]()
                    w = seg.w
                    pg, pgb = psum_next()
                    pu, pub = psum_next()
                    for kc in range(KC):
                        mm(pg[:, 0:w], blks[kc][0], seg.nb(kc), kc == 0, kc == KC - 1, [blks[kc][1], seg.nbbufs[kc]], [pgb])
                    for kc in range(KC):
                        mm(pu[:, 0:w], blks[16 + kc][0], seg.nb(kc), kc == 0, kc == KC - 1,
                           [blks[16 + kc][1], seg.nbbufs[kc]], [pub])
                    t, tb = tmp_next()
                    act(t[:, 0:w], pg[:, 0:w], AF.Silu, [pgb], [tb])
                    dve("tensor_tensor", [pub, tb], [seg.abufs[m]], out=seg.a(m), in0=pu[:, 0:w], in1=t[:, 0:w], op=ALU.mult)
                ws.release()
                if ("in", m) in hooks:
                    hooks[("in", m)]()
            warm_sqrt()
            for f in range(KC):
                blks = ws.take(MC)
                for seg in segs:
                    w = seg.w
                    po, pob = psum_next()
                    for kc in range(MC):
                        mm(po[:, 0:w], blks[kc][0], seg.a(kc), kc == 0, kc == MC - 1, [blks[kc][1], seg.abufs[kc]], [pob])
                    dve("scalar_tensor_tensor", [pob, seg.hbufs[f]], [seg.hbufs[f]], out=seg.h(f), in0=po[:, 0:w], scalar=0.5,
                        in1=seg.h(f), op0=ALU.mult, op1=ALU.add)
                    if inline_next is not None and seg is segs[0]:
                        mid_step(seg, f)
                ws.release()
                if ("out", f) in hooks:
                    hooks[("out", f)]()
            if inline_next is not None:
                mid_finish(segs[0], inline_next)

        def vg(blk):
            return Rbf(4 * blk, 4)

        zbt = vg

        def mixer(tile, with_halo):
            seg_m = seg_ms[tile % 2]
            hT, hb = seg_m.hT, seg_m.hbufs
            if with_halo:
                rmsnorm(seg_h, V_MIX)
            for cg in range(4):
                wide = [ws.take_wide() for _ in range(KC)]
                pre = None
                if cg == 0:
                    pre = [psum_next() for _ in range(4)]
                    for kc in range(KC):
                        for blk in range(4):
                            mm(pre[blk][0][:, :], nbT[:, kc, blk * 128:(blk + 1) * 128], wide[kc][0], kc == 0, False,
                               [nbb[kc], wide[kc][1]], [pre[blk][1]])
                for blk in range(4):
                    if pre is not None:
                        ps, pb = pre[blk]
                    else:
                        ps, pb = psum_next()
                        for kc in range(KC):
                            mm(ps[:, :], nbT[:, kc, blk * 128:(blk + 1) * 128], wide[kc][0], kc == 0, False,
                               [nbb[kc], wide[kc][1]], [pb])
                    mm(ps[:, :], sel_bf[:, cg * 128:(cg + 1) * 128], brow_bf[:, :], False, True, [B["sel"], B["brow"]], [pb])
                    t, tb = tmp_next()
                    act(t[:, :], ps[:, :], AF.Gelu, [pb], [tb])
                    dve("bn_stats", [tb], [vstb[blk]], out=vstats[:, blk, cg, :], in_=t[:, :])
                    dve("tensor_copy", [tb], [Rb[4 * blk + cg]], out=vg(blk)[:, cg * 512:(cg + 1) * 512], in_=t[:, :])
                ws.release()
            for blk in range(4):
                dve("bn_aggr", [vstb[blk]], [mvb[blk]], out=mv[:, blk, :], in_=vstats[:, blk, :, :].rearrange("p a b -> p (a b)"))
                act(rstd[:, blk:blk + 1], mv[:, blk, 1:2], AF.Ln, [mvb[blk], B["cst"]], [mvb[blk]], bias=cst[:, 1:2])
                act(rstd[:, blk:blk + 1], rstd[:, blk:blk + 1], AF.Exp, [mvb[blk]], [mvb[blk]], scale=-0.5)
                dve("tensor_scalar", Rb[4 * blk:4 * blk + 4] + [mvb[blk]], Rb[4 * blk:4 * blk + 4], out=vg(blk), in0=vg(blk),
                    scalar1=mv[:, blk, 0:1], scalar2=rstd[:, blk:blk + 1], op0=ALU.subtract, op1=ALU.mult)
            pend = None
            for cc in range(KC + 1):
                cur = None
                if cc < KC:
                    blks = ws.take(16)
                    pu, pub = psum_next()
                    for kc in range(KC):
                        mm(pu[:, :], blks[kc][0], nbT[:, kc, :], kc == 0, kc == KC - 1, [blks[kc][1], nbb[kc]], [pub])
                    cur = (cc, pu, pub)
                if pend is not None:
                    c0, pu0, pub0 = pend
                    h = c0 // 2
                    pg, pgb = psum_next()
                    for blk in range(4):
                        mm(pg[:, blk * 128:(blk + 1) * 128], vg(blk)[:, c0 * 128:(c0 + 1) * 128], wm_bf[:, h, :], True, True,
                           [Rb[4 * blk + c0 // 4], B["wm"]], [pgb])
                    tu, tub = tmp_next()
                    act(tu[:, :], pu0[:, :], AF.Gelu, [pub0, B["vecs"]], [tub], bias=vecs[:, V_BU + c0:V_BU + c0 + 1])
                    tsn, tsb = tmp_next()
                    dve("scalar_tensor_tensor", [pgb, B["vecs"], B["beta"]], [tsb],
                        out=tsn[:, :].rearrange("p (a c) -> p a c", a=4), in0=pg[:, :].rearrange("p (a c) -> p a c", a=4),
                        scalar=vecs[:, V_LNG + c0:V_LNG + c0 + 1], in1=beta[:, c0, :].unsqueeze(1).broadcast_to([128, 4, 128]),
                        op0=ALU.mult, op1=ALU.add)
                    dve("tensor_tensor", [tub, tsb], [Rb[16 + c0]], out=Rbf(16 + c0), in0=tu[:, :], in1=tsn[:, :], op=ALU.mult)
                pend = cur
                ws.release()
            for cg in range(4):
                wide = [ws.take_wide() for _ in range(KC)]
                for blk in range(4):
                    ps, pb = psum_next()
                    for kc in range(KC):
                        mm(ps[:, :], nbT[:, kc, blk * 128:(blk + 1) * 128], wide[kc][0], kc == 0, False,
                           [nbb[kc], wide[kc][1]], [pb])
                    mm(ps[:, :], sel_bf[:, (4 + cg) * 128:(5 + cg) * 128], brow_bf[:, :], False, True, [B["sel"], B["brow"]], [pb])
                    if blk % 2 == 0:
                        act(zbt(blk)[:, cg * 512:(cg + 1) * 512], ps[:, :], AF.Copy, [pb], [Rb[4 * blk + cg]])
                    else:
                        dve("tensor_copy", [pb], [Rb[4 * blk + cg]], out=zbt(blk)[:, cg * 512:(cg + 1) * 512], in_=ps[:, :])
                if with_halo:
                    ps, pb = psum_next()
                    for kc in range(KC):
                        mm(ps[0:HW, :], nbT_h[:, kc, :], wide[kc][0], kc == 0, False, [nbb_h[kc], wide[kc][1]], [pb])
                    mm(ps[0:HW, :], sel_bf[:, (4 + cg) * 128:(4 + cg) * 128 + HW], brow_bf[:, :], False, True,
                       [B["sel"], B["brow"]], [pb])
                    act(zb_prev[0:HW, cg * 512:(cg + 1) * 512], ps[0:HW, :], AF.Copy, [pb], [B["zb_prev"]])
                ws.release()
            for cc in range(KC):
                g = cc // 4
                pp, ppb = psum_next()
                for blk in range(4):
                    cur = zbt(blk)[:, cc * 128:(cc + 1) * 128]
                    curb = Rb[4 * blk + g]
                    if blk == 0:
                        prv, prvb = zb_prev[:, cc * 128:(cc + 1) * 128], B["zb_prev"]
                    else:
                        prv, prvb = zbt(blk - 1)[:, cc * 128:(cc + 1) * 128], Rb[4 * (blk - 1) + g]
                    if tile == 0 and blk == 0:
                        ml = [(cur, curb, PThi[:, g, :], B["PThi"]), (cur, curb, PTlo[:, g, :], B["PTlo"]),
                              (prv, prvb, PThi[:, 4 + g, :], B["PThi"]), (prv, prvb, PTlo[:, 4 + g, :], B["PTlo"])]
                    else:
                        ml = [(cur, curb, PTstd[:, g, :], B["PTstd"]), (prv, prvb, PTstd[:, 4 + g, :], B["PTstd"])]
                    for i, (l, lb, r, rb) in enumerate(ml):
                        mm(pp[:, blk * 128:(blk + 1) * 128], l, r, i == 0, i == len(ml) - 1, [lb, rb], [ppb])
                if cc % 2 == 0:
                    act(Rbf(32 + cc), pp[:, :], AF.Copy, [ppb], [Rb[32 + cc]])
                else:
                    dve("tensor_copy", [ppb], [Rb[32 + cc]], out=Rbf(32 + cc), in_=pp[:, :])
            if tile < NT - 1:
                dve("tensor_copy", Rb[12:16], [B["zb_prev"]], out=zb_prev[:, :], in_=zbt(3))
            for dd in range(KC):
                g = dd // 4
                blks = ws.take(4)
                pm, pmb = psum_next()
                for kc in range(4):
                    mm(pm[:, :], blks[kc][0], Rbf(32 + g * 4 + kc), kc == 0, kc == 3, [blks[kc][1], Rb[32 + g * 4 + kc]], [pmb])
                act(Rbf(dd), pm[:, :], AF.Copy, [pmb, B["vecs"]], [Rb[dd]], scale=vecs[:, V_PSC + dd:V_PSC + dd + 1])
                ws.release()
            for f in range(KC):
                blks = ws.take(32)
                pga, pgab = psum_next()
                for kc in range(KC):
                    mm(pga[:, :], blks[kc][0], nbT[:, kc, :], kc == 0, kc == KC - 1, [blks[kc][1], nbb[kc]], [pgab])
                pya, pyab = psum_next()
                for kc in range(KC):
                    mm(pya[:, :], blks[16 + kc][0], Rbf(16 + kc), kc == 0, kc == KC - 1, [blks[16 + kc][1], Rb[16 + kc]], [pyab])
                blks = ws.take(32)
                pgb_, pgbb = psum_next()
                for kc in range(KC):
                    mm(pgb_[:, :], blks[kc][0], nbT[:, kc, :], kc == 0, kc == KC - 1, [blks[kc][1], nbb[kc]], [pgbb])
                pyb, pybb = psum_next()
                for kc in range(KC):
                    mm(pyb[:, :], blks[16 + kc][0], Rbf(kc), kc == 0, kc == KC - 1, [blks[16 + kc][1], Rb[kc]], [pybb])
                ws.release()
                t1, t1b = tmp_next()
                act(t1[:, :], pga[:, :], AF.Sigmoid, [pgab, B["vecs"]], [t1b], bias=vecs[:, V_BGA + f:V_BGA + f + 1])
                dve("tensor_tensor", [t1b, pyab], [t1b], out=t1[:, :], in0=pya[:, :], in1=t1[:, :], op=ALU.mult)
                t2, t2b = tmp_next()
                act(t2[:, :], pgb_[:, :], AF.Sigmoid, [pgbb, B["vecs"]], [t2b], bias=vecs[:, V_BGB + f:V_BGB + f + 1])
                dve("tensor_tensor", [t2b, pybb], [t2b], out=t2[:, :], in0=pyb[:, :], in1=t2[:, :], op=ALU.mult)
                dve("tensor_tensor", [t1b, t2b], [Rb[32 + f]], out=Rbf(32 + f), in0=t1[:, :], in1=t2[:, :], op=ALU.add)
            warm_sqrt()
            for f in range(KC):
                blks = ws.take(16)
                po, pob = psum_next()
                for kc in range(KC):
                    mm(po[:, :], blks[kc][0], Rbf(32 + kc), kc == 0, kc == KC - 1, [blks[kc][1], Rb[32 + kc]], [pob])
                dve("tensor_tensor", [pob, hb[f]], [hb[f]], out=hT[:, f, :], in0=po[:, :], in1=hT[:, f, :], op=ALU.add)
                mid_step(seg_m, f)
                ws.release()
            mid_finish(seg_m, V_FFN2)

        stores = []

        def load_x(tile):
            p = tile % 2
            dma("sp", hTs[p][:, :, :], xT[:, tile * T:(tile + 1) * T].rearrange("(k p) n -> p k n", p=128), xsems[p], [], hbs[p])

        def prenorm(tile):
            rmsnorm(seg_ms[tile % 2], V_FFN1)

        fin = {}

        def fin_sq(tile, kc):
            seg = seg_ms[tile % 2]
            u = 44 + kc % 4
            act(Rbf(u), seg.h(kc), AF.Square, [seg.hbufs[kc]], [Rb[u]])

        def fin_red(tile, kc):
            if kc == 0:
                fin["bank"] = psum_reserve()
            i = fin["bank"]
            u = 44 + kc % 4
            mm(pss[i][:, :], ones_bf[:, :], Rbf(u), kc == 0, kc == KC - 1, [B["ones_bf"], Rb[u]], [psb[i]])

        def fin_squares(tile, bt):
            for kc in range(4 * bt, 4 * bt + 4):
                fin_sq(tile, kc)

        def fin_reduce(tile, bt):
            for kc in range(4 * bt, 4 * bt + 4):
                fin_red(tile, kc)

        PIECE_END = {3: (0, 0), 7: (1, 4), 11: (2, 8), 13: (3, 12), 15: (4, 14)}

        def fin_finish(tile, split=False):
            p = tile % 2
            seg = seg_ms[p]
            i = fin["bank"]
            act(rs_f[:, :], pss[i][:, :], AF.Ln, [psb[i], B["cst"]], [B["rs_f"]], scale=1.0 / D, bias=cst[:, 0:1])
            reserved.discard(i)
            act(rs_f[:, :], rs_f[:, :], AF.Exp, [B["rs_f"]], [B["rs_f"]], scale=-0.5)
            ov = outT[:, tile * T:(tile + 1) * T].rearrange("(k p) n -> p k n", p=128)
            for kc in range(KC):
                dve("scalar_tensor_tensor", [seg.hbufs[kc], B["rs_f"], B["vecs"]], [seg.hbufs[kc]], out=seg.h(kc),
                    in0=seg.h(kc), scalar=vecs[:, V_FIN + kc:V_FIN + kc + 1], in1=rs_f[:, :], op0=ALU.mult, op1=ALU.mult)
                if split and kc in PIECE_END:
                    q, c0 = PIECE_END[kc]
                    stores.append(dma("sp", ov[:, c0:kc + 1, :], hTs[p][:, c0:kc + 1, :],
                                      osems[p] if q == 0 else lsems[q - 1], seg.hbufs[c0:kc + 1], []))
            if not split:
                stores.append(dma("sp", ov, hTs[p][:, :, :], osems[p], seg.hbufs, []))

        def fin_hooks(tile, hooks, key, first, after=None):
            def mk(j):
                def f():
                    if j >= 1:
                        fin_reduce(tile, j - 1)
                    if j <= 3:
                        fin_squares(tile, j)
                    if j == 4:
                        fin_finish(tile)
                        if after is not None:
                            after()
                return f
            for j in range(5):
                hooks[(key, first + j)] = mk(j)

        warm_sqrt()
        prenorm(0)
        for tile in range(NT):
            first = tile == 0
            last = tile == NT - 1
            seg_m = seg_ms[tile % 2]
            hooks1 = {}
            if first:
                hooks1[("in", 1)] = setup_part2

                def h0():
                    setup_part3()
                    load_x(1)
                hooks1[("in", 3)] = h0
            else:
                fin_hooks(tile - 1, hooks1, "in", 1,
                          after=(lambda tile=tile: load_x(tile + 1)) if tile + 1 < NT else None)
            ffn([seg_m, seg_h] if first else [seg_m], V_FFN1, do_norm=False, hooks=hooks1,
                lazy={1: lambda: rmsnorm(seg_h, V_FFN1)} if first else None, inline_next=V_MIX, interleave=first)
            mixer(tile, first)
            hooks2 = {}
            if not last:
                nxt = seg_ms[(tile + 1) % 2]
                hooks2[("out", 1)] = lambda nxt=nxt: rms_squares(nxt)

                def h3(nxt=nxt):
                    rms_reduce(nxt)
                    rms_apply(nxt, V_FFN1)
                hooks2[("out", 3)] = h3
            else:
                for f in range(KC):
                    def hf(f=f, tile=tile):
                        if f >= 1:
                            fin_red(tile, f - 1)
                        fin_sq(tile, f)
                    hooks2[("out", f)] = hf
            ffn([seg_m], V_FFN2, do_norm=False, hooks=hooks2, interleave=True)
        fin_red(NT - 1, KC - 1)
        fin_finish(NT - 1, split=True)
        assert ws.pos == ITEMS_PER_PASS * SLOT_BLKS * NT, ws.pos
        P.add("sp", None, extra_deps=stores)
        P.emit(nc, block, sems)
    return nc


def _blocks(w):
    k, n = w.shape
    return w.reshape(k // 128, 128, n // 128, 128).transpose(0, 2, 1, 3)


def _ffn_stream(w_in, w_out):
    b = _blocks(w_in)
    s1 = np.stack([b[:, :MC], b[:, MC:]], axis=0).transpose(2, 0, 1, 3, 4).reshape(-1, 128, 128)
    s2 = _blocks(w_out).transpose(1, 0, 2, 3).reshape(-1, 128, 128)
    return [s1, s2]


def _build_wstream(inp):
    w_in = inp["w_in"]
    parts = _ffn_stream(inp["ffn1_w_in"], inp["ffn1_w_out"])
    bv = _blocks(w_in[:, 2048:4096])
    parts.append(bv.reshape(16, 4, 4, 128, 128).transpose(1, 0, 2, 3, 4).reshape(-1, 128, 128))
    parts.append(_blocks(w_in[:, 0:2048]).transpose(1, 0, 2, 3).reshape(-1, 128, 128))
    bz = _blocks(w_in[:, 4096:6144])
    parts.append(bz.reshape(16, 4, 4, 128, 128).transpose(1, 0, 2, 3, 4).reshape(-1, 128, 128))
    pw = inp["pool_w"].reshape(4, 4, 128, 4, 128).transpose(0, 3, 1, 2, 4)
    parts.append(pw.reshape(-1, 128, 128))
    ga = _blocks(w_in[:, 6144:8192]).transpose(1, 0, 2, 3)
    gb = _blocks(w_in[:, 8192:10240]).transpose(1, 0, 2, 3)
    ya = _blocks(inp["w_branch_a"]).transpose(1, 0, 2, 3)
    yb = _blocks(inp["w_branch_b"]).transpose(1, 0, 2, 3)
    parts.append(np.stack([ga, ya, gb, yb], axis=1).reshape(-1, 128, 128))
    parts.append(_blocks(inp["w_out"]).transpose(1, 0, 2, 3).reshape(-1, 128, 128))
    parts += _ffn_stream(inp["ffn2_w_in"], inp["ffn2_w_out"])
    allb = np.concatenate(parts, axis=0)
    assert allb.shape[0] == ITEMS_PER_PASS * SLOT_BLKS, allb.shape
    return np.ascontiguousarray(
        allb.reshape(ITEMS_PER_PASS, SLOT_BLKS, 128, 128).transpose(0, 2, 1, 3).reshape(ITEMS_PER_PASS, 128, SLOT_BLKS * 128))


def _pool_mats(core):
    pm = np.zeros((128, 16, 128), np.float32)
    t = np.arange(128)
    for g, w in enumerate(POOL_WINDOWS):
        for kind in range(4):
            m = np.zeros((128, 128), np.float32)
            for tt in t:
                gpos = tt if core == 0 else tt + 1000000
                cnt = min(gpos + 1, w) if kind >= 2 else w
                for dj in range(w):
                    src = tt - dj
                    if kind in (0, 2):
                        if src >= 0:
                            m[src, tt] += 1.0 / cnt
                    elif kind == 1:
                        if src < 0:
                            m[128 + src, tt] += 1.0 / cnt
                    else:
                        if src < 0 and core > 0:
                            m[HW + src, tt] += 1.0 / cnt
                if kind in (0, 2):
                    m[tt, tt] -= 1.0
            pm[:, kind * 4 + g, :] = m
    return pm.reshape(128, 16 * 128)


def _feat(v):
    return np.asarray(v, np.float32).reshape(16, 128).T


def kernel(**inp):
    inp = {k: np.asarray(v) for k, v in inp.items()}
    x = inp["x"].reshape(SEQ, D).astype(np.float32, copy=False)
    wstream = _build_wstream(inp)
    b_in = inp["b_in"].astype(np.float32)
    vecs = np.ascontiguousarray(np.concatenate([
        _feat(inp["ffn1_norm"]), _feat(inp["mix_norm"]), _feat(inp["ffn2_norm"]), _feat(inp["final_norm"]),
        _feat(b_in[0:2048]), _feat(b_in[6144:8192]), _feat(b_in[8192:10240]),
        _feat(inp["sgu_ln_g"]), _feat(inp["sgu_ln_b"]), _feat(inp["pool_scale"])], axis=1))
    brow8 = np.concatenate([b_in[2048:4096].reshape(4, 512), b_in[4096:6144].reshape(4, 512)], axis=0)
    brow = np.ascontiguousarray(np.concatenate([brow8, brow8], axis=0))
    sel = np.zeros((16, 8, 128), np.float32)
    for q in range(8):
        sel[q, q, :] = 1.0
        sel[8 + q, q, :] = 1.0
    sel = sel.reshape(16, 8 * 128)
    mask16 = np.zeros((16, 2), np.float32)
    mask16[:8, 0] = 1.0
    mask16[8:, 1] = 1.0
    wsT = np.ascontiguousarray(inp["sgu_w_s"].astype(np.float32).transpose(2, 0, 1)).reshape(128, 8 * 128)
    bsrow = np.ascontiguousarray(inp["sgu_b_s"].astype(np.float32).reshape(8 * 128))
    nc = build_program()
    in_maps = []
    for c in range(NCORES):
        xs = x[c * TOK:(c + 1) * TOK]
        if c == 0:
            halo = np.zeros((HW, D), np.float32)
        else:
            halo = x[c * TOK - HW:c * TOK]
        in_maps.append({
            "xT": np.ascontiguousarray(xs.T), "xh": np.ascontiguousarray(halo.T), "wstream": wstream, "vecs": vecs,
            "brow": brow, "sel": sel, "mask16": mask16, "wsT": wsT, "bsrow": bsrow, "pmats": _pool_mats(c),
        })
    res = run_bass_kernel_spmd(nc, in_maps, core_ids=list(range(NCORES)))
    out = np.empty((SEQ, D), np.float32)
    for c in range(NCORES):
        out[c * TOK:(c + 1) * TOK] = res.results[c]["outT"].T
    return out.reshape(1, SEQ, D)
```
